# Optimizing a Trainium2 kernel written in Bass

```python
import math
import jax, jax.numpy as jnp
from jax import lax
import numpy as np

D_MODEL = 1024
BATCH = 2
SEQ = 8192
DEPTH = 2
DEC_BATCH = 128
DEC_SEQ = 4
PAST_LEN = 2048
PAGE_SIZE = 128

D_MIX = D_MODEL
HG_WIDTH = D_MIX // 2
HG_KDIM = 128
HG_HEADS = HG_WIDTH // HG_KDIM
HG_VDIM = HG_WIDTH // HG_HEADS
FOX_WIDTH = D_MIX - HG_WIDTH
FOX_HEAD_DIM = 64
FOX_HEADS = FOX_WIDTH // FOX_HEAD_DIM
HG_CHUNK = 64
Q_BLOCK = 128
NORM_EPS = 1e-6
IN_COLS = 4 * HG_WIDTH + 4 * FOX_WIDTH + FOX_HEADS

kernel_name = "hymba_hgrn2_fox_decoder_step"


def rmsnorm(x, w):
    xf = x.astype(jnp.float32)
    y = xf * lax.rsqrt(jnp.mean(xf * xf, axis=-1, keepdims=True) + NORM_EPS)
    return (y * w.astype(jnp.float32)).astype(x.dtype)


def mixer_inputs(h, w_in_l, b_f_l, lb_l):
    B, T = h.shape[0], h.shape[1]
    f32 = jnp.float32
    proj = jnp.einsum('btd,dc->btc', h, w_in_l)
    sizes = [HG_WIDTH] * 4 + [FOX_WIDTH] * 3 + [FOX_HEADS, FOX_WIDTH]
    offs = [int(o) for o in np.cumsum(sizes)[:-1]]
    hq, hf, hi, hgate, fq, fk, fv, ff, fgate = jnp.split(proj, offs, axis=-1)
    z = hf.reshape(B, T, HG_HEADS, HG_KDIM).astype(f32)
    lb = lb_l.reshape(HG_HEADS, HG_KDIM).astype(f32)
    hg_logf = jnp.logaddexp(jnp.log(lb), jnp.log1p(-lb) + jax.nn.log_sigmoid(z))
    hg_k = (1.0 - lb) * jax.nn.sigmoid(-z)
    hg_q = hq.reshape(B, T, HG_HEADS, HG_KDIM).astype(f32) * (HG_KDIM ** -0.5)
    hg_v = hi.reshape(B, T, HG_HEADS, HG_VDIM).astype(f32)
    fox_logf = jax.nn.log_sigmoid(ff.astype(f32) + b_f_l.astype(f32))
    fox_q = fq.reshape(B, T, FOX_HEADS, FOX_HEAD_DIM)
    fox_k = fk.reshape(B, T, FOX_HEADS, FOX_HEAD_DIM)
    fox_v = fv.reshape(B, T, FOX_HEADS, FOX_HEAD_DIM)
    return hg_q, hg_k, hg_logf, hg_v, hgate, fox_q, fox_k, fox_v, fox_logf, fgate


def hgrn_chunked(q, k, log_f, v, s0):
    B, T, H, K = q.shape
    V = v.shape[-1]
    C = min(HG_CHUNK, T)
    n = T // C

    def to_chunks(a):
        return jnp.moveaxis(a.reshape(B, n, C, *a.shape[2:]), 1, 0)

    causal = jnp.tril(jnp.ones((C, C), dtype=bool))[None, :, :, None, None]

    def step(S, inp):
        qc, kc, gc, vc = inp
        b = jnp.cumsum(gc, axis=1)
        diff = b[:, :, None] - b[:, None, :]
        decay = jnp.exp(jnp.where(causal, diff, -jnp.inf))
        A = jnp.einsum('bthk,bshk,btshk->bhts', qc, kc, decay)
        o_intra = jnp.einsum('bhts,bshv->bthv', A, vc)
        o_inter = jnp.einsum('bthk,bhkv->bthv', qc * jnp.exp(b), S)
        b_last = b[:, -1]
        k_dec = kc * jnp.exp(b_last[:, None] - b)
        S_new = jnp.exp(b_last)[..., None] * S + jnp.einsum('bshk,bshv->bhkv', k_dec, vc)
        return S_new, o_intra + o_inter

    S_fin, o = lax.scan(step, s0.astype(jnp.float32),
                        (to_chunks(q), to_chunks(k), to_chunks(log_f), to_chunks(v)))
    o = jnp.moveaxis(o, 0, 1).reshape(B, T, H, V)
    return o, S_fin


def fox_prompt(q, k, v, log_f):
    B, T, H, D = q.shape
    scale = D ** -0.5
    c = jnp.cumsum(log_f, axis=1).transpose(0, 2, 1)
    kpos = jnp.arange(T)
    nb = T // Q_BLOCK

    def block(i):
        start = i * Q_BLOCK
        qb = lax.dynamic_slice_in_dim(q, start, Q_BLOCK, axis=1)
        cb = lax.dynamic_slice_in_dim(c, start, Q_BLOCK, axis=2)
        s = jnp.einsum('bqhd,bkhd->bhqk', qb, k).astype(jnp.float32) * scale
        s = s + (cb[..., None] - c[:, :, None, :])
        qpos = start + jnp.arange(Q_BLOCK)
        s = jnp.where(qpos[:, None] >= kpos[None, :], s, -jnp.inf)
        p = jax.nn.softmax(s, axis=-1)
        return jnp.einsum('bhqk,bkhd->bqhd', p.astype(v.dtype), v)

    o = lax.map(block, jnp.arange(nb))
    return jnp.moveaxis(o, 0, 1).reshape(B, T, H, D)


def fox_sample(q, k_new, v_new, logf_new, k_past, v_past, logf_past):
    P = k_past.shape[1]
    Tn = q.shape[1]
    D = q.shape[-1]
    k = jnp.concatenate([k_past.astype(k_new.dtype), k_new], axis=1)
    v = jnp.concatenate([v_past.astype(v_new.dtype), v_new], axis=1)
    lf = jnp.concatenate([logf_past.astype(jnp.float32), logf_new], axis=1)
    c = jnp.cumsum(lf, axis=1).transpose(0, 2, 1)
    cq = c[:, :, P:]
    s = jnp.einsum('bqhd,bkhd->bhqk', q, k).astype(jnp.float32) * (D ** -0.5)
    s = s + (cq[..., None] - c[:, :, None, :])
    qpos = P + jnp.arange(Tn)
    kpos = jnp.arange(P + Tn)
    s = jnp.where(qpos[:, None] >= kpos[None, :], s, -jnp.inf)
    p = jax.nn.softmax(s, axis=-1)
    return jnp.einsum('bhqk,bkhd->bqhd', p.astype(v.dtype), v)


def mixer_output(hg_o, hgate, fox_o, fgate, hg_norm_w_l, w_out_l, dtype):
    B, T = hg_o.shape[0], hg_o.shape[1]
    hg = rmsnorm(hg_o, hg_norm_w_l).reshape(B, T, HG_WIDTH)
    hg = hg * jax.nn.silu(hgate.astype(jnp.float32))
    fx = fox_o.reshape(B, T, FOX_WIDTH).astype(jnp.float32) * jax.nn.silu(fgate.astype(jnp.float32))
    merged = jnp.concatenate([hg, fx], axis=-1).astype(dtype)
    return jnp.einsum('btc,cd->btd', merged, w_out_l)


def setup_inputs(seed: int = 0) -> dict:
    key = jax.random.key(seed)
    ks = jax.random.split(key, 16)
    n_pages = PAST_LEN // PAGE_SIZE
    n_used = DEC_BATCH * n_pages
    n_pool = n_used + n_used // 4
    f32 = jnp.float32
    x_prompt = jax.random.normal(ks[0], (BATCH, SEQ, D_MODEL), f32)
    x_sample = jax.random.normal(ks[1], (DEC_BATCH, DEC_SEQ, D_MODEL), f32)
    cache_k = jax.random.normal(ks[2], (DEPTH, n_pool, PAGE_SIZE, FOX_HEADS, FOX_HEAD_DIM), f32)
    cache_v = jax.random.normal(ks[3], (DEPTH, n_pool, PAGE_SIZE, FOX_HEADS, FOX_HEAD_DIM), f32)
    cache_logf = jax.nn.log_sigmoid(2.0 + jax.random.normal(ks[4], (DEPTH, n_pool, PAGE_SIZE, FOX_HEADS), f32))
    state_hgrn = 0.3 * jax.random.normal(ks[5], (DEPTH, DEC_BATCH, HG_HEADS, HG_KDIM, HG_VDIM), f32)
    page_table = jax.random.permutation(ks[6], n_pool)[:n_used].reshape(DEC_BATCH, n_pages).astype(jnp.int32)
    norm_w = 1.0 + 0.05 * jax.random.normal(ks[7], (DEPTH, D_MODEL), f32)
    w_in = jax.random.normal(ks[8], (DEPTH, D_MODEL, IN_COLS), f32) * D_MODEL ** -0.5
    b_fox_f = 2.0 + 0.5 * jax.random.normal(ks[9], (DEPTH, FOX_HEADS), f32)
    hg_lb = 0.1 * jax.random.normal(ks[10], (DEPTH, HG_WIDTH), f32)
    hg_norm_w = 1.0 + 0.05 * jax.random.normal(ks[11], (DEPTH, HG_VDIM), f32)
    w_out = jax.random.normal(ks[12], (DEPTH, D_MIX, D_MODEL), f32) * D_MIX ** -0.5
    final_norm_w = 1.0 + 0.05 * jax.random.normal(ks[13], (D_MODEL,), f32)
    return {"x_prompt": x_prompt, "x_sample": x_sample, "cache_k": cache_k, "cache_v": cache_v,
            "cache_logf": cache_logf, "state_hgrn": state_hgrn, "page_table": page_table,
            "norm_w": norm_w, "w_in": w_in, "b_fox_f": b_fox_f, "hg_lb": hg_lb,
            "hg_norm_w": hg_norm_w, "w_out": w_out, "final_norm_w": final_norm_w}


def reference(x_prompt, x_sample, cache_k, cache_v, cache_logf, state_hgrn, page_table,
              norm_w, w_in, b_fox_f, hg_lb, hg_norm_w, w_out, final_norm_w):
    n_pages = page_table.shape[1]
    past = n_pages * PAGE_SIZE
    dbat = x_sample.shape[0]
    soft = jax.nn.softmax(hg_lb.astype(jnp.float32), axis=0)
    cum = jnp.cumsum(soft, axis=0)
    lower_bounds = cum - cum[0:1]

    xp, xs = x_prompt, x_sample
    pk_l, pv_l, plf_l, ps_l = [], [], [], []
    sk_l, sv_l, slf_l, ss_l = [], [], [], []
    for l in range(DEPTH):
        hp = rmsnorm(xp, norm_w[l])
        hq, hk, hlf, hv, hgate, fq, fk, fv, flf, fgate = mixer_inputs(hp, w_in[l], b_fox_f[l], lower_bounds[l])
        s0 = jnp.zeros((xp.shape[0], HG_HEADS, HG_KDIM, HG_VDIM), jnp.float32)
        hg_o, S_p = hgrn_chunked(hq, hk, hlf, hv, s0)
        fox_o = fox_prompt(fq, fk, fv, flf)
        xp = xp + mixer_output(hg_o, hgate, fox_o, fgate, hg_norm_w[l], w_out[l], xp.dtype)
        pk_l.append(fk); pv_l.append(fv); plf_l.append(flf); ps_l.append(S_p)
        hs = rmsnorm(xs, norm_w[l])
        sq, sk, slf, sv, sgate, gq, gk, gv, glf, ggate = mixer_inputs(hs, w_in[l], b_fox_f[l], lower_bounds[l])
        hg_os, S_s = hgrn_chunked(sq, sk, slf, sv, state_hgrn[l])
        k_past = cache_k[l, page_table].reshape(dbat, past, FOX_HEADS, FOX_HEAD_DIM)
        v_past = cache_v[l, page_table].reshape(dbat, past, FOX_HEADS, FOX_HEAD_DIM)
        lf_past = cache_logf[l, page_table].reshape(dbat, past, FOX_HEADS)
        fox_os = fox_sample(gq, gk, gv, glf, k_past, v_past, lf_past)
        xs = xs + mixer_output(hg_os, sgate, fox_os, ggate, hg_norm_w[l], w_out[l], xs.dtype)
        sk_l.append(gk); sv_l.append(gv); slf_l.append(glf); ss_l.append(S_s)

    y_prompt = rmsnorm(xp, final_norm_w)
    y_sample = rmsnorm(xs, final_norm_w)
    prompt_k = jnp.stack(pk_l); prompt_v = jnp.stack(pv_l)
    prompt_logf = jnp.stack(plf_l); prompt_state_hgrn = jnp.stack(ps_l)
    sample_k = jnp.stack(sk_l); sample_v = jnp.stack(sv_l)
    sample_logf = jnp.stack(slf_l); sample_state_hgrn = jnp.stack(ss_l)
    return (y_prompt, y_sample, prompt_k, prompt_v, prompt_logf, prompt_state_hgrn,
            sample_k, sample_v, sample_logf, sample_state_hgrn)
```

```python
import numpy as np
import concourse.bass as bass
import concourse.mybir as mybir
from concourse.bass_utils import run_bass_kernel_spmd

F32 = mybir.dt.float32
BF16 = mybir.dt.bfloat16
I32 = mybir.dt.int32
AF = mybir.ActivationFunctionType
ALU = mybir.AluOpType

D = 1024
NCOL = 1026
TOK_LOC = 2112
GTOK = 8448
EPS = 1e-6
NPOOL = 2560
RG = [[0, 1, 2, 3], [4, 5, 6, 7]]
C_HQ, C_HF, C_HGT, C_FQ, C_FK, C_FGT, C_HI, C_FV, C_FF = 0, 128, 256, 384, 512, 640, 768, 896, 1024
K_ID, K_U, K_L, K_ONE, K_MASK = 0, 128, 256, 384, 512
NCONST = 512 + 2048


CFG = {"nlayers": 2, "nblocks": 16, "hg": True, "fox": True, "sample": True, "C": True, "final": True, "trace": False}


PSUM_KEYS = {"pA", "pB", "pC", "pD", "pE", "pF", "pG", "pH"}


class Sched:
    def __init__(self, nc, sems, dma_sems, cc_sems):
        self.nc = nc
        self.q = {e: [] for e in ("tensor", "vector", "scalar", "gpsimd", "sync")}
        self.sem = sems
        self.cnt = {e: 0 for e in self.q}
        self.lastw = {}
        self.readers = {}
        self.seen = {e: {} for e in self.q}
        self.dma_sems = dma_sems
        self.dma_tot = [0] * len(dma_sems)
        self.dma_i = 0
        self.dma_ij = {}
        self.pe_nowait = True
        self.cc_sems = cc_sems
        self.cc_tot = [0] * len(cc_sems)
        self.cc_i = 0
        self.nsem = {}

    def _wait(self, eng, sem, val):
        if self.pe_nowait and CFG.get("pe_nowait", True) and eng == "tensor" and sem is self.sem["tensor"]:
            return
        key = id(sem)
        self.nsem[key] = sem
        if self.seen[eng].get(key, 0) >= val:
            return
        self.seen[eng][key] = val
        self.q[eng].append(lambda e, s=sem, v=val: e.wait_ge(s, v))

    def _deps(self, eng, reads, writes):
        for k in reads:
            w = self.lastw.get(k)
            if w is not None:
                self._wait(eng, w[0], w[1])
        for k in writes:
            w = self.lastw.get(k)
            if w is not None:
                self._wait(eng, w[0], w[1])
            for (s, v) in self.readers.get(k, {}).values():
                self._wait(eng, s, v)

    def _record(self, sem, val, reads, writes):
        for k in writes:
            self.lastw[k] = (sem, val)
            self.readers[k] = {}
        for k in reads:
            self.readers.setdefault(k, {})[id(sem)] = (sem, val)

    def op(self, eng, fn, reads=(), writes=()):
        pr = tuple(k for k in reads if k in PSUM_KEYS and k not in writes)
        if pr:
            writes = tuple(writes) + pr
        self._deps(eng, reads, writes)
        self.cnt[eng] += 1
        v = self.cnt[eng]
        s = self.sem[eng]
        self.q[eng].append(lambda e, f=fn, s=s: f(e).then_inc(s, 1))
        self._record(s, v, reads, writes)

    def dma(self, eng, fn, reads=(), writes=()):
        half = len(self.dma_sems) // 2
        base = 0 if eng == "sync" else half
        j = self.dma_ij.get(eng, 0)
        self.dma_ij[eng] = (j + 1) % half
        i = base + j
        s = self.dma_sems[i]
        self._deps(eng, reads, writes)
        if self.dma_tot[i] > 0:
            self._wait(eng, s, self.dma_tot[i])
        self.dma_tot[i] += 16
        self.q[eng].append(lambda e, f=fn, s=s: f(e).then_inc(s, 16))
        self._record(s, self.dma_tot[i], reads, writes)

    def raw(self, eng, fn):
        self.q[eng].append(fn)

    def cc(self, fn, reads=(), writes=()):
        if not CFG.get("cc", True):
            return
        eng = "gpsimd"
        i = self.cc_i
        self.cc_i = (self.cc_i + 1) % len(self.cc_sems)
        s = self.cc_sems[i]
        self._deps(eng, reads, writes)
        if self.cc_tot[i] > 0:
            self._wait(eng, s, self.cc_tot[i])
        self.cc_tot[i] += 1
        self.q[eng].append(lambda e, f=fn, s=s: f(e).then_inc(s, 1))
        self._record(s, self.cc_tot[i], reads, writes)

    def pe_fence(self):
        v = self.cnt["tensor"]
        if v > 0:
            s = self.sem["tensor"]
            if self.seen["tensor"].get(id(s), 0) < v:
                self.seen["tensor"][id(s)] = v
                self.q["tensor"].append(lambda e, s=s, v=v: e.wait_ge(s, v))

    def barrier(self):
        for eng in self.q:
            self.final_waits(eng)

    def final_waits(self, eng):
        for i, s in enumerate(self.cc_sems):
            if self.cc_tot[i] > 0:
                self._wait(eng, s, self.cc_tot[i])
        for i, s in enumerate(self.dma_sems):
            if self.dma_tot[i] > 0:
                self._wait(eng, s, self.dma_tot[i])
        for e2 in self.q:
            if self.cnt[e2] > 0 and e2 != eng:
                self._wait(eng, self.sem[e2], self.cnt[e2])


def build():
    nc = bass.Bass("TRN2", target_bir_lowering=False)

    def din(name, shape, dt=F32):
        return nc.dram_tensor(name, list(shape), dt, kind="ExternalInput")

    def dout(name, shape, dt=F32):
        return nc.dram_tensor(name, list(shape), dt, kind="ExternalOutput")

    x_loc = din("x_loc", [TOK_LOC, D])
    w_in = din("w_in", [2, D, NCOL])
    w_out = din("w_out", [2, D, D])
    nw_bc = din("nw_bc", [3, 128, D])
    vecs = din("vecs", [128, 8])
    bfrow = din("bfrow", [2, 2])
    bfb = din("bfb", [128, 4])
    cst = din("cst", [128, NCONST])
    st_in = din("st_in", [2, 64, 128, 128])
    npool = NPOOL if CFG["sample"] else 1
    ckv = din("ckv", [2, npool, 128, 258])
    ptab = din("ptab", [128, 1024], I32)
    rank_in = din("rank_in", [1, 2], I32)

    y_loc = dout("y_loc", [TOK_LOC, D])
    o_kT = dout("o_kT", [2, 2, 64, GTOK])
    o_v = dout("o_v", [2, GTOK, 128])
    o_lf = dout("o_lf", [2, 2, GTOK])
    o_sp = dout("o_sp", [2, 128, 128])
    o_ss = dout("o_ss", [2, 64, 128, 128])

    hT_in = [nc.dram_tensor(f"hT_in{c}", [D, 512 if c < 4 else 64], BF16) for c in range(5)]
    hT_out = [nc.dram_tensor(f"hT_out{c}", [4 * D, 512 if c < 4 else 64], BF16) for c in range(5)]
    mg_in = [nc.dram_tensor(f"mg_in{c}", [256, 2048], BF16) for c in range(4)]
    mg_all = nc.dram_tensor("mg_all", [4, 1024, 2048], BF16)
    mgs_in = nc.dram_tensor("mgs_in", [256, 256], BF16)
    mgs_out = nc.dram_tensor("mgs_out", [1024, 256], BF16)
    xbuf = [nc.dram_tensor(f"xbuf{i}", [TOK_LOC, D], F32) for i in range(2)]
    cscr = nc.dram_tensor("cscr", [16, 2, 512], F32)

    import contextlib
    es = contextlib.ExitStack()
    with es:
        def sb(name, shape, dt=F32):
            return es.enter_context(nc.sbuf_tensor(name, list(shape), dt))

        def ps(name, shape, dt=F32):
            return es.enter_context(nc.psum_tensor(name, list(shape), dt))

        sems = {e: es.enter_context(nc.semaphore("s_" + e)) for e in ("tensor", "vector", "scalar", "gpsimd", "sync")}
        dma_sems = [es.enter_context(nc.semaphore(f"dq{i}")) for i in range(24)]
        cc_sems = [es.enter_context(nc.semaphore(f"cq{i}")) for i in range(4)]
        S = Sched(nc, sems, dma_sems, cc_sems)

        xt = sb("xt", [128, D])
        cstt = sb("cstt", [128, 512])
        maskbf = sb("maskbf", [128, 2048], BF16)
        identbf = sb("identbf", [128, 128], BF16)
        nwt = sb("nwt", [128, D])
        vect = sb("vect", [128, 8])
        lbt = sb("lbt", [128, 4])
        bfr = sb("bfr", [2, 2])
        nbfr = sb("nbfr", [2, 2])
        bfbt = sb("bfbt", [128, 4])
        ptt = sb("ptt", [128, 1024], I32)
        ptf = sb("ptf", [128, 1024])
        idxt = [sb(f"idxt{i}", [128, 1024], I32) for i in range(2)]
        rkt = sb("rkt", [1, 2], I32)
        hf = sb("hf", [128, D])
        junk = hf
        ssq = sb("ssq", [128, 2])
        hTt = sb("hTt", [128, 8, 128], BF16)
        wst0 = sb("wst0", [128, NCOL])
        wst = [wst0, wst0]
        wbf = sb("wbf", [128, 8, NCOL], BF16)
        hblk = sb("hblk", [128, 8, 512], BF16)
        KT = [sb(f"KT{h}", [66, 8192], BF16) for h in range(2)]
        QT = [sb(f"QT{h}", [66, 512], BF16) for h in range(2)]
        VA = sb("VA", [128, 64, 2, 65], BF16)
        negc = sb("negc", [128, 2, 64, 1])
        qTs = sb("qTs", [128, 512])
        fTs = sb("fTs", [128, 512])
        lfTs = sb("lfTs", [128, 512])
        kTs = sb("kTs", [128, 512])
        sgH = sb("sgH", [128, 512])
        sgF = [sb(f"sgF{h}", [64, 512]) for h in range(2)]
        tmpA = sb("tmpA", [128, 512])
        tmpB = sb("tmpB", [128, 512])
        kst = sb("kst", [64, 512])
        vst = sb("vst", [128, 128])
        vhg = sb("vhg", [64, 8, 128], BF16)
        frow = sb("frow", [2, 512])
        lrow = sb("lrow", [2, 512])
        crow = sb("crow", [2, 512])
        onesrow = sb("onesrow", [2, 512])
        clast = sb("clast", [2, 2])
        hirow = sb("hirow", [2, 512], BF16)
        lorow = sb("lorow", [2, 512], BF16)
        bT = sb("bT", [128, 64])
        ones64 = sb("ones64", [128, 64])
        negr = sb("negr", [128, 2])
        ek = sb("ek", [128, 64])
        eq = sb("eq", [128, 64])
        eb = sb("eb", [128, 64])
        ed = sb("ed", [128, 64])
        qtl = sb("qtl", [128, 64], BF16)
        ktl = sb("ktl", [128, 64], BF16)
        qht = sb("qht", [128, 64], BF16)
        kdT = sb("kdT", [128, 64])
        kd = sb("kd", [64, 128], BF16)
        ATm = sb("ATm", [64, 64], BF16)
        Sst = sb("Sst", [128, 128])
        Sbf = sb("Sbf", [128, 128], BF16)
        oacc = sb("oacc", [128, 512])
        sqt = sb("sqt", [128, 512])
        hgm = sb("hgm", [128, 512], BF16)
        PT = [sb(f"PT{i}", [128, 512], BF16) for i in range(3)]
        recrow = sb("recrow", [65, 512])
        recb = sb("recb", [64, 512])
        fxm = sb("fxm", [64, 512], BF16)
        mT = sb("mT", [128, 8, 128], BF16)
        KVp = sb("KVp", [128, 16, 258])
        KTp = sb("KTp", [128, 2048], BF16)
        VAp = sb("VAp", [128, 16, 2, 65], BF16)
        cft = sb("cft", [128, 16, 2])
        lat = sb("lat", [128, 16, 2])
        bia = sb("bia", [128, 16, 2])
        bia4 = sb("bia4", [128, 32, 4])
        one16 = sb("one16", [128, 16])
        stmp = sb("stmp", [128, 128])
        PTs = sb("PTs", [128, 128], BF16)
        qTj = sb("qTj", [128, 256], BF16)
        kTj = sb("kTj", [128, 256], BF16)
        tk = sb("tk", [4, 386])
        sgt = sb("sgt", [4, 128])
        vnew = sb("vnew", [4, 2, 65], BF16)
        vhs = sb("vhs", [4, 128], BF16)
        l1c = sb("l1c", [4, 2])
        bnew = sb("bnew", [4, 2])
        PTn = sb("PTn", [4, 2, 4], BF16)
        PTnf = sb("PTnf", [4, 2, 4])
        osm = sb("osm", [4, 128])
        rcs = sb("rcs", [4, 2])
        fxT = sb("fxT", [128, 256], BF16)

        pA = ps("pA", [128, 512])
        pB = ps("pB", [128, 512])
        pC = ps("pC", [128, 512])
        pD = ps("pD", [128, 512])
        pE = ps("pE", [128, 512])
        pF = ps("pF", [128, 512])
        pG = ps("pG", [128, 512])
        pH = ps("pH", [128, 512])

        ident = cstt[:, K_ID:K_ID + 128]
        Uinc = cstt[:, K_U:K_U + 128]
        Lstr = cstt[:, K_L:K_L + 128]
        onesf = cstt[:, K_ONE:K_ONE + 128]

        def dma(eng, out, in_, reads, writes, slow=False):
            if slow:
                S.dma(eng, lambda e: e.dma_start(out=out, in_=in_, allow_slow_non_contiguous=True), reads, writes)
            else:
                S.dma(eng, lambda e: e.dma_start(out=out, in_=in_), reads, writes)

        def act(out, in_, func, reads, writes, bias=0.0, scale=1.0, accum_out=None):
            if accum_out is None:
                S.op("scalar", lambda e: e.activation(out=out, in_=in_, func=func, bias=bias, scale=scale), reads, writes)
            else:
                S.op("scalar", lambda e: e.activation(out=out, in_=in_, func=func, bias=bias, scale=scale,
                                                       accum_out=accum_out), reads, writes)

        def ts(out, in0, s1, s2, op0, op1, reads, writes, eng="vector"):
            if s2 is None:
                if op0 == ALU.mult:
                    S.op(eng, lambda e: e.tensor_scalar(out=out, in0=in0, scalar1=s1, scalar2=0.0, op0=op0, op1=ALU.add), reads, writes)
                else:
                    S.op(eng, lambda e: e.tensor_scalar(out=out, in0=in0, scalar1=s1, scalar2=1.0, op0=op0, op1=ALU.mult), reads, writes)
            else:
                S.op(eng, lambda e: e.tensor_scalar(out=out, in0=in0, scalar1=s1, scalar2=s2, op0=op0, op1=op1), reads, writes)

        def tt(out, in0, in1, op, reads, writes, eng="vector"):
            S.op(eng, lambda e: e.tensor_tensor(out=out, in0=in0, in1=in1, op=op), reads, writes)

        def stt(out, in0, scalar, in1, op0, op1, reads, writes):
            S.op("vector", lambda e: e.scalar_tensor_tensor(out=out, in0=in0, scalar=scalar, in1=in1, op0=op0, op1=op1),
                 reads, writes)

        def cp(out, in_, reads, writes, eng="vector"):
            if eng == "scalar":
                S.op("scalar", lambda e: e.copy(out=out, in_=in_), reads, writes)
            else:
                S.op(eng, lambda e: e.tensor_copy(out=out, in_=in_), reads, writes)

        def mm(out, lhsT, rhs, start, stop, reads, writes):
            S.op("tensor", lambda e: e.matmul(out, lhsT, rhs, start=start, stop=stop), reads, writes)

        def tr(out, in_, idn, reads, writes):
            S.op("tensor", lambda e: e.transpose(out, in_, idn), reads, writes)

        def scan(out, d0, d1, init, reads, writes):
            S.op("vector", lambda e: e.tensor_tensor_scan(out=out, data0=d0, data1=d1, initial=init,
                                                           op0=ALU.mult, op1=ALU.add), reads, writes)

        def recip(out, in_, reads, writes):
            S.op("vector", lambda e: e.reciprocal(out=out, in_=in_), reads, writes)

        def memset(ap, val, writes, eng="vector"):
            S.op(eng, lambda e: e.memset(ap, val), (), writes)

        dma("sync", cstt[:], cst[:, 0:512], (), ("cstt",))
        for hh in range(2):
            dma("sync", hf[:, :], cst[:, 512 + 1024 * hh:512 + 1024 * hh + 1024], (), ("hf",))
            ts(maskbf[:, 1024 * hh:1024 * hh + 1024], hf[:, :], 30000.0, -30000.0, ALU.mult, ALU.add, ("hf",), ("maskbf",))
        dma("sync", vect[:], vecs[:, :], (), ("vect",))
        dma("sync", bfr[:], bfrow[:, :], (), ("bfr",))
        dma("sync", bfbt[:], bfb[:, :], (), ("bfbt",))
        dma("sync", ptt[:], ptab[:, :], (), ("ptt",))
        dma("sync", rkt[:], rank_in[:, :], (), ("rkt",))
        cp(identbf[:], cstt[:, K_ID:K_ID + 128], ("cstt",), ("identbf",))
        memset(VA[:], 1.0, ("VA",))
        memset(VAp[:], 1.0, ("VAp",))
        memset(vnew[:], 1.0, ("vnew",))
        memset(onesrow[:], 1.0, ("onesrow",))
        memset(ones64[:], 1.0, ("ones64",))
        memset(one16[:], 1.0, ("one16",))
        for h in range(2):
            memset(KT[h][64:66, :], 1.0, (f"KT{h}",))
        ts(nbfr[:], bfr[:], -1.0, None, ALU.mult, None, ("bfr",), ("nbfr",))
        cp(ptf[:], ptt[:], ("ptt",), ("ptf",))
        ts(ptf[:], ptf[:], 128.0, vect[:, 4:5], ALU.mult, ALU.add, ("ptf", "vect"), ("ptf",))
        cp(idxt[0][:], ptf[:], ("ptf",), ("idxt",))
        ts(ptf[:], ptf[:], float(NPOOL * 128), None, ALU.add, None, ("ptf",), ("ptf",))
        cp(idxt[1][:], ptf[:], ("ptf",), ("idxt",))
        ckv_rows = ckv.ap().rearrange("l n p d -> (l n p) d")
        memset(lbt[:, 0:1], 0.0, ("lbt",))
        memset(lbt[:, 2:3], 1.0, ("lbt",))
        tt(lbt[:, 1:2], vect[:, 0:1], vect[:, 1:2], ALU.subtract, ("vect",), ("lbt",))
        act(lbt[:, 1:2], lbt[:, 1:2], AF.Exp, ("lbt",), ("lbt",))
        ts(lbt[:, 1:2], lbt[:, 1:2], 1.0, None, ALU.add, None, ("lbt",), ("lbt",))
        recip(lbt[:, 1:2], lbt[:, 1:2], ("lbt",), ("lbt",))
        ts(lbt[:, 3:4], lbt[:, 1:2], -1.0, 1.0, ALU.mult, ALU.add, ("lbt",), ("lbt",))

        regs = {}

        def mk_regs(e):
            regs["rank"] = es.enter_context(e.register("rank"))
            regs["rank64"] = es.enter_context(e.register("rank64"))
            regs["pg"] = es.enter_context(e.register("pg"))

        def rms_tile(i, rows, src, skey):
            x = xt[:rows, :]
            dma("sync", x, src[128 * i:128 * i + rows, :], (skey,), ("xs",))
            act(junk[:rows, :], x, AF.Square, ("xs", ), ("hf", "ssq"), accum_out=ssq[:rows, 0:1])
            ts(ssq[:rows, 1:2], ssq[:rows, 0:1], 1.0 / D, EPS, ALU.mult, ALU.add, ("ssq",), ("ssq1",))
            act(ssq[:rows, 1:2], ssq[:rows, 1:2], AF.Ln, ("ssq1",), ("ssq1",))
            act(ssq[:rows, 1:2], ssq[:rows, 1:2], AF.Exp, ("ssq1",), ("ssq1",), scale=-0.5)
            stt(hf[:rows, :], x, ssq[:rows, 1:2], nwt[:rows, :], ALU.mult, ALU.mult, ("xs", "ssq1", "nwt"), ("hf",))

        def stage_A(l):
            dma("sync", nwt[:], nw_bc[l, :, :], (), ("nwt",))
            for i in range(CFG.get("A_tiles", 17)):
                rows = 128 if i < 16 else 64
                rms_tile(i, rows, (x_loc if l == 0 else xbuf[0]), ("xloc" if l == 0 else "xbuf0"))
                for dc in range(8):
                    pt = pA if dc < 4 else pB
                    tr(pt[:, (dc % 4) * 128:(dc % 4) * 128 + rows], hf[:rows, dc * 128:(dc + 1) * 128],
                       ident[:rows, :rows], ("hf", "cstt"), ("pA" if dc < 4 else "pB",))
                cp(hTt[:, 0:4, :rows], pA[:, :].rearrange("p (a t) -> p a t", a=4)[:, :, :rows], ("pA",), ("hTt",), eng="scalar")
                cp(hTt[:, 4:8, :rows], pB[:, :].rearrange("p (a t) -> p a t", a=4)[:, :, :rows], ("pB",), ("hTt",), eng="vector")
                c = i // 4
                co = (i % 4) * 128
                dst = hT_in[c][:, co:co + rows].rearrange("(dc p) t -> p dc t", p=128)
                dma("sync", dst, hTt[:, :, :rows], ("hTt",), (f"hTin{c}",))
            for c in range(5):
                S.cc(lambda e, c=c: e.collective_compute("AllGather", ALU.bypass, replica_groups=RG,
                                                         ins=[hT_in[c].ap().opt()], outs=[hT_out[c].ap().opt()]),
                     (f"hTin{c}",), (f"hTout{c}",))

        def load_w(src_rows_fn, ncols):
            for dc in range(8):
                st = wst[0]
                k = "wst0"
                dma("gpsimd", st[:, :ncols], src_rows_fn(dc), (), (k,))
                cp(wbf[:, dc, :ncols], st[:, :ncols], (k,), ("wbf",), eng=("vector" if dc % 2 else "scalar"))

        def proj_fm(pt, pkey, c0, M, T):
            for dc in range(8):
                mm(pt[:M, :T], wbf[:, dc, c0:c0 + M], hblk[:, dc, :T], dc == 0, dc == 7, ("wbf", "hblk"), (pkey,))

        def silu_from(pt, pkey, M, T, out, okey, mulcol=None):
            act(tmpA[:M, :T], pt[:M, :T], AF.Exp, (pkey,), ("tmpA",), scale=-1.0)
            ts(tmpA[:M, :T], tmpA[:M, :T], 1.0, None, ALU.add, None, ("tmpA",), ("tmpA",))
            recip(tmpA[:M, :T], tmpA[:M, :T], ("tmpA",), ("tmpA",))
            tt(out, pt[:M, :T], tmpA[:M, :T], ALU.mult, (pkey, "tmpA"), (okey,))
            if mulcol is not None:
                ts(out, out, mulcol, None, ALU.mult, None, (okey, "vect"), (okey,))

        def hg_features(l, T):
            proj_fm(pA, "pA", C_HQ, 128, T)
            act(qTs[:, :T], pA[:, :T], AF.Copy, ("pA",), ("qTs",), scale=128 ** -0.5)
            proj_fm(pB, "pB", C_HF, 128, T)
            act(tmpB[:, :T], pB[:, :T], AF.Exp, ("pB",), ("tmpB",), scale=-1.0)
            ts(tmpB[:, :T], tmpB[:, :T], 1.0, None, ALU.add, None, ("tmpB",), ("tmpB",))
            recip(tmpB[:, :T], tmpB[:, :T], ("tmpB",), ("tmpB",))
            ts(fTs[:, :T], tmpB[:, :T], lbt[:, 2 + l:3 + l], lbt[:, l:l + 1], ALU.mult, ALU.add, ("tmpB", "lbt"), ("fTs",))
            act(lfTs[:, :T], fTs[:, :T], AF.Ln, ("fTs",), ("lfTs",))
            ts(kTs[:, :T], fTs[:, :T], -1.0, 1.0, ALU.mult, ALU.add, ("fTs",), ("kTs",))
            proj_fm(pC, "pC", C_HGT, 128, T)
            silu_from(pC, "pC", 128, T, sgH[:, :T], "sgH", mulcol=vect[:, 2 + l:3 + l])

        def ff_rows(l, T, tok0, first):
            proj_fm(pD, "pD", C_FF, 2, T)
            act(frow[:, :T], pD[:2, :T], AF.Exp, ("pD", "nbfr"), ("frow",), scale=-1.0, bias=nbfr[:, l:l + 1])
            act(frow[:, :T], frow[:, :T], AF.Ln, ("frow",), ("frow",), bias=1.0)
            ts(lrow[:, :T], frow[:, :T], -1.0, None, ALU.mult, None, ("frow",), ("lrow",))
            dma("gpsimd", o_lf[l, :, tok0:tok0 + T], lrow[:, :T], ("lrow",), ("o_lf",))

        def hg_chunk(l, c0, C, vT_ap, vkey, mid):
            cs = slice(c0, c0 + C)
            scan(bT[:, :C], ones64[:, :C], lfTs[:, cs], 0.0, ("ones64", "lfTs"), ("bT",))
            ts(negr[:, 0:1], bT[:, mid:mid + 1], -1.0, None, ALU.mult, None, ("bT",), ("negr",))
            ts(ek[:, :C], bT[:, :C], -1.0, bT[:, mid:mid + 1], ALU.mult, ALU.add, ("bT",), ("ek",))
            ts(ek[:, :C], ek[:, :C], 41.0, None, ALU.min, None, ("ek",), ("ek",))
            act(ek[:, :C], ek[:, :C], AF.Exp, ("ek",), ("ek",))
            ts(eq[:, :C], bT[:, :C], negr[:, 0:1], 41.0, ALU.add, ALU.min, ("bT", "negr"), ("eq",))
            act(eq[:, :C], eq[:, :C], AF.Exp, ("eq",), ("eq",))
            act(eb[:, :C], bT[:, :C], AF.Exp, ("bT",), ("eb",))
            act(ed[:, :C], bT[:, :C], AF.Exp, ("bT",), ("ed",), scale=-1.0, bias=bT[:, C - 1:C])
            yield
            tt(qtl[:, :C], qTs[:, cs], eq[:, :C], ALU.mult, ("qTs", "eq"), ("qtl",))
            tt(ktl[:, :C], kTs[:, cs], ek[:, :C], ALU.mult, ("kTs", "ek"), ("ktl",))
            tt(qht[:, :C], qTs[:, cs], eb[:, :C], ALU.mult, ("qTs", "eb"), ("qht",))
            tt(kdT[:, :C], kTs[:, cs], ed[:, :C], ALU.mult, ("kTs", "ed"), ("kdT",), eng="gpsimd")
            yield
            mm(pE[:C, :C], ktl[:, :C], qtl[:, :C], True, True, ("ktl", "qtl"), ("pE",))
            tt(ATm[:C, :C], pE[:C, :C], Uinc[:C, :C], ALU.mult, ("pE", "cstt"), ("ATm",))
            tr(pF[:C, :128], kdT[:, :C], ident[:, :], ("kdT", "cstt"), ("pF",))
            cp(kd[:C, :], pF[:C, :128], ("pF",), ("kd",), eng="scalar")
            yield
            mm(pG[:, :C], vT_ap, ATm[:C, :C], True, False, (vkey, "ATm"), ("pG",))
            mm(pG[:, :C], Sbf[:, :], qht[:, :C], False, True, ("Sbf", "qht"), ("pG",))
            cp(oacc[:, cs], pG[:, :C], ("pG",), ("oacc",), eng="scalar")
            yield
            mm(pH[:, :128], kd[:C, :], vT_ap, True, True, ("kd", vkey), ("pH",))
            stt(Sst[:, :], Sst[:, :], eb[:, C - 1:C], pH[:, :128], ALU.mult, ALU.add, ("Sst", "eb", "pH"), ("Sst",))
            cp(Sbf[:, :], Sst[:, :], ("Sst",), ("Sbf",), eng="gpsimd")
            yield

        def hg_gate_out(l, T, dst_ap, dkey):
            act(sqt[:, :T], oacc[:, :T], AF.Square, ("oacc",), ("sqt",))
            mm(pE[:, :T], onesf, sqt[:, :T], True, True, ("cstt", "sqt"), ("pE",))
            ts(sqt[:, :T], pE[:, :T], 1.0 / 128, EPS, ALU.mult, ALU.add, ("pE",), ("sqt",))
            act(sqt[:, :T], sqt[:, :T], AF.Ln, ("sqt",), ("sqt",))
            act(sqt[:, :T], sqt[:, :T], AF.Exp, ("sqt",), ("sqt",), scale=-0.5)
            tt(sqt[:, :T], sqt[:, :T], oacc[:, :T], ALU.mult, ("sqt", "oacc"), ("sqt",))
            tt(hgm[:, :T], sqt[:, :T], sgH[:, :T], ALU.mult, ("sqt", "sgH"), ("hgm",))
            dma("gpsimd", dst_ap, hgm[:, :T], ("hgm",), (dkey,))

        def stage_B_block(l, gb):
            T = 512
            tok0 = gb * 512
            rr, c = gb // 4, gb % 4
            src = hT_out[c][rr * D:(rr + 1) * D, :].rearrange("(dc p) t -> p dc t", p=128)
            dma("sync", hblk[:, :, :], src, (f"hTout{c}",), ("hblk",))
            if CFG.get("bstop", 99) < 1:
                return
            hg_features(l, T)
            if CFG.get("bstop", 99) < 2:
                return
            ff_rows(l, T, tok0, gb == 0)
            if CFG.get("bstop", 99) < 3:
                return
            if gb == 0:
                scan(crow[:, :T], onesrow[:, :T], lrow[:, :T], 0.0, ("onesrow", "lrow"), ("crow",))
            else:
                scan(crow[:, :T], onesrow[:, :T], lrow[:, :T], clast[:, 0:1], ("onesrow", "lrow", "clast"), ("crow",))
            cp(clast[:, 0:1], crow[:, T - 1:T], ("crow",), ("clast",))
            if CFG.get("bstop", 99) < 4:
                return
            ts(frow[:, :T], crow[:, :T], 8.0, None, ALU.mult, None, ("crow",), ("frow",))
            cp(hirow[:, :T], frow[:, :T], ("frow",), ("hirow",))
            tt(lorow[:, :T], frow[:, :T], hirow[:, :T], ALU.subtract, ("frow", "hirow"), ("lorow",))
            for h in range(2):
                dma("sync", QT[h][64:65, :T], hirow[h:h + 1, :T], ("hirow",), (f"QT{h}",))
                dma("sync", QT[h][65:66, :T], lorow[h:h + 1, :T], ("lorow",), (f"QT{h}",))
            if CFG.get("bstop", 99) < 5:
                return
            if CFG.get("bstop", 99) == 5.6:
                memset(negc[:, :, 4 * gb:4 * gb + 4, :], 0.0, ("negc",))
                return
            if CFG.get("bstop", 99) == 5.7:
                dma("gpsimd", cscr[gb], crow[:, :T], ("crow",), ("cscr",))
                return
            ts(frow[:, :T], crow[:, :T], -1.0, None, ALU.mult, None, ("crow",), ("frow",))
            dma("gpsimd", cscr[gb], frow[:, :T], ("frow",), ("cscr",))
            for h in range(2):
                dma("gpsimd", negc[:, h, 4 * gb:4 * gb + 4, :], cscr[gb, h].rearrange("(j p x) -> p j x", p=128, x=1), ("cscr",), ("negc",), slow=True)

            if CFG.get("bstop", 99) < 6:
                return
            for h in range(2):
                proj_fm(pE, "pE", C_FQ + 64 * h, 64, T)
                bs = CFG.get("bstop", 99)
                if bs == 6.05:
                    continue
                cp(QT[h][0:64, :T], pE[:64, :T], ("pE",), (f"QT{h}",), eng="scalar")
                if bs == 6.1:
                    continue
                proj_fm(pF, "pF", C_FK + 64 * h, 64, T)
                cp(KT[h][0:64, tok0:tok0 + T], pF[:64, :T], ("pF",), (f"KT{h}",), eng="scalar")
                if bs == 6.2:
                    continue
                cp(kst[:, :T], pF[:64, :T], ("pF",), ("kst",), eng="vector")
                if bs == 6.3:
                    continue
                dma("gpsimd", o_kT[l, h, :, tok0:tok0 + T], kst[:, :T], ("kst",), ("o_kT",))
                if bs == 6.4:
                    continue
                proj_fm(pG, "pG", C_FGT + 64 * h, 64, T)
                silu_from(pG, "pG", 64, T, sgF[h][:, :T], f"sgF{h}")
            if CFG.get("bstop", 99) < 7:
                return
            for t4 in range(4):
                kt = 4 * gb + t4
                for dc in range(8):
                    mm(pH[:, :128], hblk[:, dc, 128 * t4:128 * t4 + 128], wbf[:, dc, C_FV:C_FV + 128], dc == 0, dc == 7,
                       ("hblk", "wbf"), ("pH",))
                cp(VA[:, kt, :, 0:64], pH[:, :128].rearrange("p (h d) -> p h d", h=2), ("pH",), ("VA",), eng="scalar")
                cp(vst[:, :], pH[:, :128], ("pH",), ("vst",), eng="vector")
                dma("gpsimd", o_v[l, tok0 + 128 * t4:tok0 + 128 * t4 + 128, :], vst[:, :], ("vst",), ("o_v",))
            for ch in range(8):
                for dc in range(8):
                    mm(pH[:64, 128:256], hblk[:, dc, 64 * ch:64 * ch + 64], wbf[:, dc, C_HI:C_HI + 128], dc == 0, dc == 7,
                       ("hblk", "wbf"), ("pH",))
                cp(vhg[:, ch, :], pH[:64, 128:256], ("pH",), ("vhg",), eng="vector")
            cmg = gb // 4
            cols = slice((gb % 4) * 512, (gb % 4) * 512 + 512)
            def hg_stream():
                for ch in range(8):
                    yield from hg_chunk(l, 64 * ch, 64, vhg[:, ch, :], "vhg", 31)
                hg_gate_out(l, T, mg_in[cmg][0:128, cols], f"mgin{cmg}")
                yield

            def att_stream():
                for h in range(2):
                    nkt = 4 * gb + 4
                    for kt in range(nkt):
                        pS = (pA, pB, pC)[kt % 3]
                        pSk = ("pA", "pB", "pC")[kt % 3]
                        P = PT[kt % 3]
                        Pk = f"PT{kt % 3}"
                        diag = kt >= 4 * gb
                        mm(pS[:, :T], KT[h][0:66, 128 * kt:128 * kt + 128], QT[h][0:66, :T], True, not diag, (f"KT{h}", f"QT{h}"), (pSk,))
                        if diag:
                            k = kt - 4 * gb
                            mm(pS[:, :T], identbf[:, :], maskbf[:, 512 * k:512 * k + 512], False, True, ("identbf", "maskbf"), (pSk,))
                        act(P[:, :T], pS[:, :T], AF.Exp, (pSk, "negc"), (Pk,), scale=0.125, bias=negc[:, h, kt, :])
                        mm(pD[:65, :T], VA[:, kt, h, :], P[:, :T], kt == 0, kt == nkt - 1, ("VA", Pk), ("pD",))
                        yield
                    recip(recrow[64:65, :T], pD[64:65, :T], ("pD",), ("recrow",))
                    mm(pA[:64, :T], onesf[64:65, 0:64], recrow[64:65, :T], True, True, ("cstt", "recrow"), ("pA",))
                    cp(recb[:, :T], pA[:64, :T], ("pA",), ("recb",), eng="scalar")
                    tt(recb[:, :T], pD[:64, :T], recb[:, :T], ALU.mult, ("pD", "recb"), ("recb",))
                    tt(fxm[:, :T], recb[:, :T], sgF[h][:, :T], ALU.mult, ("recb", f"sgF{h}"), ("fxm",))
                    dma("gpsimd", mg_in[cmg][128 + 64 * h:192 + 64 * h, cols], fxm[:, :T], ("fxm",), (f"mgin{cmg}",))
                    yield

            streams = []
            if CFG["hg"]:
                streams.append(hg_stream())
            if CFG["fox"]:
                streams.append(att_stream())
            if not CFG.get("interleave", True):
                for g in streams:
                    for _ in g:
                        pass
                streams = []
            while streams:
                for g in list(streams):
                    try:
                        next(g)
                    except StopIteration:
                        streams.remove(g)
            if gb % 4 == 3 and CFG["C"]:
                S.cc(lambda e, c=cmg: e.collective_compute("AllGather", ALU.bypass, replica_groups=RG,
                                                           ins=[mg_in[c].ap().opt()], outs=[mg_all[c].opt()]),
                     (f"mgin{cmg}",), ("mgall",))

        def stage_B_sample(l):
            T = 256
            tok0 = 8192
            for rr in range(4):
                src = hT_out[4][rr * D:(rr + 1) * D, :].rearrange("(dc p) t -> p dc t", p=128)
                dma("sync", hblk[:, :, 64 * rr:64 * rr + 64], src, ("hTout4",), ("hblk",))
            hg_features(l, T)
            ff_rows(l, T, tok0, True)
            proj_fm(pE, "pE", C_FQ, 128, T)
            cp(qTj[:, :T], pE[:, :T], ("pE",), ("qTj",), eng="scalar")
            proj_fm(pF, "pF", C_FK, 128, T)
            cp(kTj[:, :T], pF[:, :T], ("pF",), ("kTj",), eng="scalar")
            for h in range(2):
                cp(kst[:, :T], pF[64 * h:64 * h + 64, :T], ("pF",), ("kst",), eng="vector")
                dma("gpsimd", o_kT[l, h, :, tok0:tok0 + T], kst[:, :T], ("kst",), ("o_kT",))
            def gather(bb_, half):
                for pg in range(8 * half, 8 * half + 8):
                    idx = bb_ * 16 + pg

                    def f(e, idx=idx, pg=pg):
                        return e.indirect_dma_start(out=KVp[:, pg, :], out_offset=None, in_=ckv_rows,
                                                    in_offset=bass.IndirectOffsetOnAxis(ap=idxt[l][:, idx:idx + 1], axis=0))
                    S.dma("gpsimd", f, ("idxt",), ("KVA" if half == 0 else "KVB",))

            def consume(half):
                hk = "KVA" if half == 0 else "KVB"
                for q4 in range(2 * half, 2 * half + 2):
                    pt = (pC, pD, pE, pF)[q4]
                    pk = ("pC", "pD", "pE", "pF")[q4]
                    for j in range(4):
                        tr(pt[:, 128 * j:128 * j + 128], KVp[:, 4 * q4 + j, 0:128], ident[:, :], (hk, "cstt"), (pk,))
                    cp(KTp[:, 512 * q4:512 * q4 + 512], pt[:, :], (pk,), ("KTp",), eng=("scalar" if q4 % 2 == 0 else "vector"))
                pgs = slice(8 * half, 8 * half + 8)
                cp(VAp[:, pgs, :, 0:64], KVp[:, pgs, 128:256].rearrange("p a (h d) -> p a h d", h=2), (hk,), ("VAp",), eng="gpsimd")
                lfh = KVp[:, pgs, 256:258]
                mm(pB[:, 32 + 16 * half:48 + 16 * half].rearrange("p (a h) -> p a h", h=2), Lstr, lfh, True, True, ("cstt", hk), ("pB",))
                mm(pB[:, 64 + 16 * half:80 + 16 * half].rearrange("p (a h) -> p a h", h=2), onesf, lfh, True, True, ("cstt", hk), ("pB",))

            gather(0, 0)
            gather(0, 1)
            for bb in range(64):
                c0 = 4 * bb
                consume(0)
                if bb + 1 < 64:
                    gather(bb + 1, 0)
                consume(1)
                if bb + 1 < 64:
                    gather(bb + 1, 1)
                for dc in range(8):
                    mm(pA[:4, :386], hblk[:, dc, c0:c0 + 4], wbf[:, dc, C_FGT:C_FGT + 386], dc == 0, dc == 7, ("hblk", "wbf"), ("pA",))
                cp(tk[:, :], pA[:4, :386], ("pA",), ("tk",), eng="scalar")
                act(sgt[:, :], tk[:, 0:128], AF.Exp, ("tk",), ("sgt",), scale=-1.0)
                ts(sgt[:, :], sgt[:, :], 1.0, None, ALU.add, None, ("sgt",), ("sgt",))
                recip(sgt[:, :], sgt[:, :], ("sgt",), ("sgt",))
                tt(sgt[:, :], sgt[:, :], tk[:, 0:128], ALU.mult, ("sgt", "tk"), ("sgt",))
                cp(vhs[:, :], tk[:, 128:256], ("tk",), ("vhs",))
                cp(vnew[:, :, 0:64], tk[:, 256:384].rearrange("p (h d) -> p h d", h=2), ("tk",), ("vnew",))
                dma("gpsimd", o_v[l, tok0 + c0:tok0 + c0 + 4, :], tk[:, 256:384], ("tk",), ("o_v",))
                tt(l1c[:, :], tk[:, 384:386], bfbt[:4, 2 * l:2 * l + 2], ALU.add, ("tk", "bfbt"), ("l1c",))
                act(l1c[:, :], l1c[:, :], AF.Exp, ("l1c",), ("l1c",), scale=-1.0)
                act(l1c[:, :], l1c[:, :], AF.Ln, ("l1c",), ("l1c",), bias=1.0)
                mm(pB[:4, 0:2], Uinc[0:4, 0:4], l1c[:, :], True, True, ("cstt", "l1c"), ("pB",))
                cp(bnew[:, :], pB[:4, 0:2], ("pB",), ("bnew",))
                dma("sync", Sst[:, :], st_in[l, bb, :, :], (), ("Sst",))
                cp(Sbf[:, :], Sst[:, :], ("Sst",), ("Sbf",), eng="gpsimd")
                for _ in hg_chunk(l, c0, 4, vhs[:, :], "vhs", 1):
                    pass
                dma("gpsimd", o_ss[l, bb, :, :], Sst[:, :], ("Sst",), ("o_ss",))
                cp(lat[:, :, :], pB[:, 64:96].rearrange("p (a h) -> p a h", h=2), ("pB",), ("lat",), eng="scalar")
                for h in range(2):
                    scan(cft[:, :, h], one16[:, :], lat[:, :, h], 0.0, ("one16", "lat"), ("cft",))
                for h in range(2):
                    ts(lat[:, :, h], cft[:, :, h], -1.0, cft[:, 15, h:h + 1], ALU.mult, ALU.add, ("cft",), ("lat",))
                tt(bia[:, :, :], pB[:, 32:64].rearrange("p (a h) -> p a h", h=2), lat[:, :, :], ALU.add, ("pB", "lat"), ("bia",))
                for h in range(2):
                    S.pe_fence()
                    for pg in range(16):
                        col = (pg * 2 + h) * 4
                        mm(pG[:, col:col + 4], KTp[64 * h:64 * h + 64, 128 * pg:128 * pg + 128], qTj[64 * h:64 * h + 64, c0:c0 + 4],
                           True, True, ("KTp", "qTj"), ("pG",))
                S.pe_fence()
                for t_ in range(4):
                    cp(bia4[:, :, t_], bia[:, :, :].rearrange("p a h -> p (a h)"), ("bia",), ("bia4",), eng=("vector" if t_ % 2 else "gpsimd"))
                stt(stmp[:, :], pG[:, 0:128], 0.125, bia4[:, :, :].rearrange("p a t -> p (a t)"), ALU.mult, ALU.add,
                    ("pG", "bia4"), ("stmp",))
                act(PTs[:, :], stmp[:, :], AF.Exp, ("stmp",), ("PTs",))
                for h in range(2):
                    S.pe_fence()
                    mm(pH[:4, 256 + 4 * h:260 + 4 * h], kTj[64 * h:64 * h + 64, c0:c0 + 4], qTj[64 * h:64 * h + 64, c0:c0 + 4],
                       True, True, ("kTj", "qTj"), ("pH",))
                    S.pe_fence()
                    act(PTnf[:, h, :], pH[:4, 256 + 4 * h:260 + 4 * h], AF.Exp, ("pH", "bnew"), ("PTnf",), scale=0.125,
                        bias=bnew[:, h:h + 1])
                    tt(PTn[:, h, :], PTnf[:, h, :], Uinc[0:4, 0:4], ALU.mult, ("PTnf", "cstt"), ("PTn",))
                for h in range(2):
                    for pg in range(16):
                        col = (pg * 2 + h) * 4
                        mm(pH[:4, 65 * h:65 * h + 65], PTs[:, col:col + 4], VAp[:, pg, h, :], pg == 0, False, ("PTs", "VAp"), ("pH",))
                    mm(pH[:4, 65 * h:65 * h + 65], PTn[:, h, :], vnew[:, h, :], False, True, ("PTn", "vnew"), ("pH",))
                for h in range(2):
                    recip(rcs[:, h:h + 1], pH[:4, 65 * h + 64:65 * h + 65], ("pH",), ("rcs",))
                    ts(osm[:, 64 * h:64 * h + 64], pH[:4, 65 * h:65 * h + 64], rcs[:, h:h + 1], None, ALU.mult, None, ("pH", "rcs"), ("osm",))
                tt(osm[:, :], osm[:, :], sgt[:, :], ALU.mult, ("osm", "sgt"), ("osm",))
                tr(pB[:, 128:132], osm[:, :], ident[0:4, 0:4], ("osm", "cstt"), ("pB",))
                cp(fxT[:, c0:c0 + 4], pB[:, 128:132], ("pB",), ("fxT",), eng="scalar")
            hg_gate_out(l, T, mgs_in[0:128, :], "mgsin")
            dma("gpsimd", mgs_in[128:256, :], fxT[:, :T], ("fxT",), ("mgsin",))
            S.cc(lambda e: e.collective_compute("AllGather", ALU.bypass, replica_groups=RG,
                                                ins=[mgs_in.ap().opt()], outs=[mgs_out.ap().opt()]),
                 ("mgsin",), ("mgsout",))

        def stage_C(l):
            load_w(lambda dc: w_out[l, dc * 128:(dc + 1) * 128, :], D)
            for i in range(17):
                rows = 128 if i < 16 else 64
                if i < 16:
                    def f(e, i=i):
                        if "rankv" not in regs:
                            e.reg_load(regs["rank"], rkt[0:1, 0:1])
                            regs["rankv"] = e.snap(regs["rank"], min_val=0, max_val=3)
                        v = regs["rankv"]
                        return e.dma_start(out=mT[:, :, :],
                                           in_=mg_all[bass.ds(v, 1), :, 128 * i:128 * i + 128][0].rearrange("(mc p) t -> p mc t", p=128))
                    S.dma("sync", f, ("mgall", "rkt"), ("mT",))
                else:
                    def f(e):
                        if "rank64v" not in regs:
                            e.reg_load(regs["rank64"], rkt[0:1, 1:2])
                            regs["rank64v"] = e.snap(regs["rank64"], min_val=0, max_val=192)
                        v = regs["rank64v"]
                        return e.dma_start(out=mT[:, :, :64],
                                           in_=mgs_out[:, bass.ds(v, 64)].rearrange("(mc p) t -> p mc t", p=128))
                    S.dma("sync", f, ("mgsout", "rkt"), ("mT",))
                xsrc = x_loc if l == 0 else xbuf[0]
                dma("sync", xt[:rows, :], xsrc[128 * i:128 * i + rows, :], (("xloc" if l == 0 else "xbuf0"),), ("xs",))
                for half in range(2):
                    pt = (pA, pB)[half]
                    pk = ("pA", "pB")[half]
                    for mc in range(8):
                        s_, e_ = mc // 2, mc % 2
                        blk = s_ if e_ == 0 else 4 + s_
                        mm(pt[:rows, :], mT[:, mc, :rows], wbf[:, blk, half * 512:half * 512 + 512], mc == 0, mc == 7, ("mT", "wbf"), (pk,))
                    tt(xt[:rows, half * 512:half * 512 + 512], pt[:rows, :], xt[:rows, half * 512:half * 512 + 512], ALU.add,
                       (pk, "xs"), ("xs",))
                dma("gpsimd", xbuf[l][128 * i:128 * i + rows, :], xt[:rows, :], ("xs",), (f"xbuf{l}",))

        def stage_final():
            dma("sync", nwt[:], nw_bc[2, :, :], (), ("nwt",))
            for i in range(17):
                rows = 128 if i < 16 else 64
                rms_tile(i, rows, xbuf[1], "xbuf1")
                r0 = 128 * i
                dma("gpsimd", y_loc[r0:r0 + rows, :], hf[:rows, :], ("hf",), ("y_loc",))

        for l in range(CFG["nlayers"]):
            stage_A(l)
            if CFG.get("stopA"):
                break
            load_w(lambda dc: w_in[l, dc * 128:(dc + 1) * 128, :], NCOL)
            memset(Sst[:, :], 0.0, ("Sst",))
            memset(Sbf[:, :], 0.0, ("Sbf",), eng="gpsimd")
            for gb in range(CFG["nblocks"]):
                stage_B_block(l, gb)
            dma("gpsimd", o_sp[l, :, :], Sst[:, :], ("Sst",), ("o_sp",))
            if CFG["sample"]:
                stage_B_sample(l)
            if CFG["C"]:
                stage_C(l)
        if CFG["final"]:
            stage_final()
        S.final_waits("gpsimd")

        with nc.Block() as block:
            @block.sync
            def _(e):
                mk_regs(e)
                for f in S.q["sync"]:
                    f(e)

            @block.tensor
            def _(e):
                for f in S.q["tensor"]:
                    f(e)

            @block.vector
            def _(e):
                for f in S.q["vector"]:
                    f(e)

            @block.scalar
            def _(e):
                for f in S.q["scalar"]:
                    f(e)

            @block.gpsimd
            def _(e):
                for f in S.q["gpsimd"]:
                    f(e)
    return nc


def _consts():
    c = np.zeros((128, NCONST), np.float32)
    p = np.arange(128)
    c[:, K_ID:K_ID + 128] = np.eye(128, dtype=np.float32)
    c[:, K_U:K_U + 128] = (p[:, None] <= p[None, :]).astype(np.float32)
    c[:, K_L:K_L + 128] = (p[:, None] > p[None, :]).astype(np.float32)
    c[:, K_ONE:K_ONE + 128] = 1.0
    j = np.arange(512)
    for k in range(4):
        c[:, K_MASK + 512 * k:K_MASK + 512 * k + 512] = ((p[:, None] + 128 * k) <= j[None, :]).astype(np.float32)
    return c


def kernel(x_prompt, x_sample, cache_k, cache_v, cache_logf, state_hgrn, page_table,
           norm_w, w_in, b_fox_f, hg_lb, hg_norm_w, w_out, final_norm_w):
    f32 = np.float32
    x_prompt = np.asarray(x_prompt, f32); x_sample = np.asarray(x_sample, f32)
    cache_k = np.asarray(cache_k, f32); cache_v = np.asarray(cache_v, f32); cache_logf = np.asarray(cache_logf, f32)
    state_hgrn = np.asarray(state_hgrn, f32); page_table = np.asarray(page_table, np.int32)
    norm_w = np.asarray(norm_w, f32); w_in = np.asarray(w_in, f32); b_fox_f = np.asarray(b_fox_f, f32)
    hg_lb = np.asarray(hg_lb, f32); hg_norm_w = np.asarray(hg_norm_w, f32); w_out = np.asarray(w_out, f32)
    final_norm_w = np.asarray(final_norm_w, f32)

    nc = build()
    cst = _consts()
    nw_bc = np.ascontiguousarray(np.broadcast_to(np.stack([norm_w[0], norm_w[1], final_norm_w])[:, None, :], (3, 128, D)))
    in_maps = []
    for c in range(8):
        g, r = c // 4, c % 4
        xl = np.concatenate([x_prompt[g, 2048 * r:2048 * r + 2048],
                             x_sample[64 * g + 16 * r:64 * g + 16 * r + 16].reshape(64, D)], axis=0)
        cols = np.concatenate([
            np.arange(0 + 128 * r, 0 + 128 * r + 128),
            np.arange(512 + 128 * r, 512 + 128 * r + 128),
            np.arange(1536 + 128 * r, 1536 + 128 * r + 128),
            np.arange(2048 + 128 * r, 2048 + 128 * r + 128),
            np.arange(2560 + 128 * r, 2560 + 128 * r + 128),
            np.arange(3592 + 128 * r, 3592 + 128 * r + 128),
            np.arange(1024 + 128 * r, 1024 + 128 * r + 128),
            np.arange(3072 + 128 * r, 3072 + 128 * r + 128),
            np.arange(3584 + 2 * r, 3584 + 2 * r + 2),
        ])
        vecs = np.zeros((128, 8), f32)
        vecs[:, 0] = hg_lb[0, 128 * r:128 * r + 128]
        vecs[:, 1] = hg_lb[1, 128 * r:128 * r + 128]
        vecs[:, 2] = hg_norm_w[0]
        vecs[:, 3] = hg_norm_w[1]
        vecs[:, 4] = np.arange(128, dtype=f32)
        bf2 = b_fox_f[:, 2 * r:2 * r + 2]
        in_maps.append({
            "x_loc": np.ascontiguousarray(xl),
            "w_in": np.ascontiguousarray(w_in[:, :, cols]),
            "w_out": w_out,
            "nw_bc": nw_bc,
            "vecs": vecs,
            "bfrow": np.ascontiguousarray(bf2.T),
            "bfb": np.ascontiguousarray(np.broadcast_to(bf2.reshape(1, 4), (128, 4))),
            "cst": cst,
            "st_in": np.ascontiguousarray(state_hgrn[:, 64 * g:64 * g + 64, r]),
            "ckv": (np.concatenate([cache_k[:, :, :, 2 * r:2 * r + 2, :].reshape(2, NPOOL, 128, 128),
                                    cache_v[:, :, :, 2 * r:2 * r + 2, :].reshape(2, NPOOL, 128, 128),
                                    cache_logf[:, :, :, 2 * r:2 * r + 2]], axis=3) if CFG["sample"]
                    else np.zeros((2, 1, 128, 258), f32)),
            "ptab": np.ascontiguousarray(np.broadcast_to(page_table[64 * g:64 * g + 64].reshape(1, 1024), (128, 1024))),
            "rank_in": np.array([[r, 64 * r]], np.int32),
        })
    res = run_bass_kernel_spmd(nc, in_maps, core_ids=list(range(8)), **({"trace": True} if CFG["trace"] else {}))
    if CFG["trace"]:
        print("EXEC_NS", res.exec_time_ns)
    R = res.results

    y_prompt = np.zeros((2, 8192, D), f32); y_sample = np.zeros((128, 4, D), f32)
    pk = np.zeros((2, 2, 8192, 8, 64), f32); pv = np.zeros((2, 2, 8192, 8, 64), f32)
    plf = np.zeros((2, 2, 8192, 8), f32); psn = np.zeros((2, 2, 4, 128, 128), f32)
    sk = np.zeros((2, 128, 4, 8, 64), f32); sv = np.zeros((2, 128, 4, 8, 64), f32)
    slf = np.zeros((2, 128, 4, 8), f32); ssn = np.zeros((2, 128, 4, 128, 128), f32)
    for c in range(8):
        g, r = c // 4, c % 4
        o = R[c]
        y_prompt[g, 2048 * r:2048 * r + 2048] = o["y_loc"][:2048]
        y_sample[64 * g + 16 * r:64 * g + 16 * r + 16] = o["y_loc"][2048:].reshape(16, 4, D)
        kT = o["o_kT"]
        v = o["o_v"].reshape(2, GTOK, 2, 64)
        lf = o["o_lf"]
        for h in range(2):
            hh = 2 * r + h
            pk[:, g, :, hh, :] = kT[:, h, :, :8192].transpose(0, 2, 1)
            sk[:, 64 * g:64 * g + 64, :, hh, :] = kT[:, h, :, 8192:].transpose(0, 2, 1).reshape(2, 64, 4, 64)
            pv[:, g, :, hh, :] = v[:, :8192, h, :]
            sv[:, 64 * g:64 * g + 64, :, hh, :] = v[:, 8192:, h, :].reshape(2, 64, 4, 64)
            plf[:, g, :, hh] = lf[:, h, :8192]
            slf[:, 64 * g:64 * g + 64, :, hh] = lf[:, h, 8192:].reshape(2, 64, 4)
        psn[:, g, r] = o["o_sp"]
        ssn[:, 64 * g:64 * g + 64, r] = o["o_ss"]
    return (y_prompt, y_sample, pk, pv, plf, psn, sk, sv, slf, ssn)
```
